# Optimizing a Trainium2 kernel written in Bass

```python
import math
import jax, jax.numpy as jnp
from jax import lax
import numpy as np

D_MODEL = 1024
BATCH = 16
SEQ = 2048
DEPTH = 2

A_HEADS = 8
A_HEAD_DIM = 64
A_WIDTH = A_HEADS * A_HEAD_DIM
A_DECAY_LORA = 64
A_ICLR_LORA = 64
A_GATE_LORA = 128
A_PROJ = 3 * A_WIDTH + A_DECAY_LORA + A_ICLR_LORA + A_GATE_LORA
A_GN_EPS = 64e-5

B_HEADS = 4
B_QK_DIM = 128
B_V_DIM = 256
B_QK_WIDTH = B_HEADS * B_QK_DIM
B_V_WIDTH = B_HEADS * B_V_DIM
B_CHUNK = 128
B_ROPE_BASE = 10000.0
B_GN_EPS = 1e-5

C_WIDTH = 512
C_GROUP = 16
C_GROUPS = C_WIDTH // C_GROUP
C_STATE = 64
C_DT_MIN = 1e-3
C_DT_MAX = 1e-1

PROJ_TOTAL = A_PROJ + 2 * B_QK_WIDTH + 2 * B_V_WIDTH + C_WIDTH + 3 * D_MODEL
PROJ_SPLITS = (A_PROJ,
               A_PROJ + B_QK_WIDTH,
               A_PROJ + 2 * B_QK_WIDTH,
               A_PROJ + 2 * B_QK_WIDTH + B_V_WIDTH,
               A_PROJ + 2 * B_QK_WIDTH + 2 * B_V_WIDTH,
               A_PROJ + 2 * B_QK_WIDTH + 2 * B_V_WIDTH + C_WIDTH)
A_SPLITS = (A_WIDTH, 2 * A_WIDTH, 3 * A_WIDTH,
            3 * A_WIDTH + A_DECAY_LORA, 3 * A_WIDTH + A_DECAY_LORA + A_ICLR_LORA)
BRANCH_ROWS = A_WIDTH + B_V_WIDTH + C_WIDTH

D_FF = 4 * D_MODEL
DN_ALPHA = (2.0 * DEPTH) ** 0.25
DN_BETA = (8.0 * DEPTH) ** -0.25
LN_EPS = 1e-5

kernel_name = 'hybrid_rwkv7_retnet_s5_deepnorm'


def _layer_norm(x, g, b):
    xf = x.astype(jnp.float32)
    mu = xf.mean(-1, keepdims=True)
    var = jnp.square(xf - mu).mean(-1, keepdims=True)
    y = (xf - mu) * lax.rsqrt(var + LN_EPS)
    return (y * g.astype(jnp.float32) + b.astype(jnp.float32)).astype(x.dtype)


def _group_norm(x, eps):
    xf = x.astype(jnp.float32)
    mu = xf.mean(-1, keepdims=True)
    var = jnp.square(xf - mu).mean(-1, keepdims=True)
    return (xf - mu) * lax.rsqrt(var + eps)


def _token_shift(z, mu):
    prev = jnp.pad(z, ((0, 0), (1, 0), (0, 0)))[:, :-1]
    return z + mu * (prev - z)


def _rwkv7_mixer(z, w0, w2, a0, a2, g2, k_k, k_a, r_k, lnx_g, lnx_b):
    bsz, seq, _ = z.shape
    r, k, v, zw, za, zg = jnp.split(z, A_SPLITS, axis=-1)
    w_ll = -jax.nn.softplus(-(w0 + jnp.tanh(zw) @ w2)) - 0.5
    a = jax.nn.sigmoid(a0 + za @ a2)
    g = jax.nn.sigmoid(zg) @ g2
    heads = lambda t: t.reshape(bsz, seq, A_HEADS, A_HEAD_DIM).astype(jnp.float32)
    r, k, v, a, w_ll = heads(r), heads(k), heads(v), heads(a), heads(w_ll)
    kk = k * k_k.astype(jnp.float32)
    kk = kk / jnp.maximum(jnp.sqrt(jnp.sum(kk * kk, -1, keepdims=True)), 1e-12)
    k = k * (1.0 + (a - 1.0) * k_a.astype(jnp.float32))
    decay = jnp.exp(-jnp.exp(w_ll))

    def step(state, inp):
        r_t, w_t, k_t, v_t, na_t, nb_t = inp
        sa = jnp.einsum('bhvk,bhk->bhv', state, na_t)
        state = (state * w_t[:, :, None, :] + sa[..., None] * nb_t[:, :, None, :]
                 + v_t[..., None] * k_t[:, :, None, :])
        return state, jnp.einsum('bhvk,bhk->bhv', state, r_t)

    xs = tuple(jnp.moveaxis(t, 1, 0) for t in (r, decay, k, v, -kk, kk * a))
    s0 = jnp.zeros((bsz, A_HEADS, A_HEAD_DIM, A_HEAD_DIM), jnp.float32)
    _, o = lax.scan(step, s0, xs)
    o = jnp.moveaxis(o, 0, 1)
    o = _group_norm(o, A_GN_EPS) * lnx_g.astype(jnp.float32) + lnx_b.astype(jnp.float32)
    o = o + jnp.sum(r * k * r_k.astype(jnp.float32), -1, keepdims=True) * v
    return (o.reshape(bsz, seq, A_WIDTH) * g.astype(jnp.float32)).astype(z.dtype)


def _rope(t, cos, sin):
    t1, t2 = jnp.split(t, 2, axis=-1)
    return jnp.concatenate([t1 * cos - t2 * sin, t2 * cos + t1 * sin], axis=-1)


def _retention_mixer(q, k, v, g):
    bsz, seq, _ = q.shape
    n_chunks = seq // B_CHUNK
    f32 = jnp.float32
    pos = jnp.arange(seq, dtype=f32)
    half = B_QK_DIM // 2
    inv_freq = B_ROPE_BASE ** (-jnp.arange(half, dtype=f32) / half)
    ang = pos[:, None] * inv_freq[None, :]
    cos, sin = jnp.cos(ang)[:, None, :], jnp.sin(ang)[:, None, :]
    q = _rope(q.reshape(bsz, seq, B_HEADS, B_QK_DIM).astype(f32), cos, sin)
    k = _rope(k.reshape(bsz, seq, B_HEADS, B_QK_DIM).astype(f32), cos, sin) * (B_QK_DIM ** -0.5)
    v = v.reshape(bsz, seq, B_HEADS, B_V_DIM).astype(f32)
    log_gamma = jnp.log(1.0 - 2.0 ** (-5.0 - jnp.arange(B_HEADS, dtype=f32)))
    idx = jnp.arange(B_CHUNK, dtype=f32)
    rel = idx[:, None] - idx[None, :]
    inner_decay = jnp.where(rel >= 0, jnp.exp(log_gamma[:, None, None] * jnp.maximum(rel, 0.0)), 0.0)
    q_decay = jnp.exp(log_gamma[:, None] * (idx + 1.0))
    k_decay = jnp.exp(log_gamma[:, None] * (B_CHUNK - 1.0 - idx))
    chunk_decay = jnp.exp(log_gamma * B_CHUNK)
    to_chunks = lambda t: t.reshape(bsz, n_chunks, B_CHUNK, B_HEADS, t.shape[-1])
    qc, kc, vc = to_chunks(q), to_chunks(k), to_chunks(v)
    scores = jnp.einsum('bnihd,bnjhd->bnhij', qc, kc) * inner_decay
    inner = jnp.einsum('bnhij,bnjhe->bnihe', scores, vc)
    upd = jnp.einsum('bnjhd,hj,bnjhe->nbhde', kc, k_decay, vc)

    def step(state, u_n):
        return chunk_decay[None, :, None, None] * state + u_n, state

    r0 = jnp.zeros((bsz, B_HEADS, B_QK_DIM, B_V_DIM), f32)
    _, r_prev = lax.scan(step, r0, upd)
    cross = jnp.einsum('bnihd,hi,nbhde->bnihe', qc, q_decay, r_prev)
    o = (inner + cross).reshape(bsz, seq, B_HEADS, B_V_DIM)
    o = _group_norm(o, B_GN_EPS).reshape(bsz, seq, B_V_WIDTH)
    return (jax.nn.silu(g.astype(f32)) * o).astype(g.dtype)


def _s5_mixer(u, lam_re, lam_im, log_dt, b_re, b_im, c_re, c_im, d_skip, w_glu, b_glu):
    bsz, seq, _ = u.shape
    f32 = jnp.float32
    uf = u.reshape(bsz, seq, C_GROUPS, C_GROUP).astype(f32)
    dt = jnp.exp(log_dt.astype(f32))[:, None]
    lr, li = lam_re.astype(f32), lam_im.astype(f32)
    mag = jnp.exp(lr * dt)
    ab_re, ab_im = mag * jnp.cos(li * dt), mag * jnp.sin(li * dt)
    den = lr * lr + li * li
    f_re = ((ab_re - 1.0) * lr + ab_im * li) / den
    f_im = (ab_im * lr - (ab_re - 1.0) * li) / den
    bre, bim = b_re.astype(f32), b_im.astype(f32)
    bb_re = f_re[..., None] * bre - f_im[..., None] * bim
    bb_im = f_re[..., None] * bim + f_im[..., None] * bre
    bu_re = jnp.einsum('gpc,bsgc->bsgp', bb_re, uf)
    bu_im = jnp.einsum('gpc,bsgc->bsgp', bb_im, uf)
    a_re = jnp.broadcast_to(ab_re, (1, seq, C_GROUPS, C_STATE))
    a_im = jnp.broadcast_to(ab_im, (1, seq, C_GROUPS, C_STATE))

    def combine(e1, e2):
        a1r, a1i, b1r, b1i = e1
        a2r, a2i, b2r, b2i = e2
        return (a2r * a1r - a2i * a1i, a2r * a1i + a2i * a1r,
                a2r * b1r - a2i * b1i + b2r, a2r * b1i + a2i * b1r + b2i)

    _, _, x_re, x_im = lax.associative_scan(combine, (a_re, a_im, bu_re, bu_im), axis=1)
    y = (jnp.einsum('gcp,bsgp->bsgc', c_re.astype(f32), x_re)
         - jnp.einsum('gcp,bsgp->bsgc', c_im.astype(f32), x_im)
         + d_skip.astype(f32) * uf)
    y = jax.nn.gelu(y.reshape(bsz, seq, C_WIDTH))
    y = y * jax.nn.sigmoid(y @ w_glu.astype(f32) + b_glu.astype(f32))
    return y.astype(u.dtype)


def setup_inputs(seed: int = 0) -> dict:
    key = jax.random.key(seed)
    ks = iter(jax.random.split(key, 40))
    f32 = jnp.float32
    nrm = lambda shape, scale: scale * jax.random.normal(next(ks), shape, f32)
    L, D = DEPTH, D_MODEL
    ratio = jnp.linspace(0.0, 1.0, A_WIDTH, dtype=f32)
    branch_scale = jnp.concatenate([
        jnp.full((A_WIDTH,), A_WIDTH ** -0.5, f32),
        jnp.full((B_V_WIDTH,), B_V_WIDTH ** -0.5, f32),
        jnp.full((C_WIDTH,), C_WIDTH ** -0.5, f32)])[:, None]
    lam_im0 = math.pi * jnp.arange(C_STATE, dtype=f32)
    return {
        'x': nrm((BATCH, SEQ, D), 1.0),
        'w_in': nrm((L, D, PROJ_TOTAL), D ** -0.5),
        'b_gate': nrm((L, 3 * D), 0.02),
        'a_shift': jax.random.uniform(next(ks), (L, A_PROJ), f32),
        'a_w0': -5.5 + 5.0 * ratio ** 0.85 + nrm((L, A_WIDTH), 0.1),
        'a_w2': nrm((L, A_DECAY_LORA, A_WIDTH), 0.1 * A_DECAY_LORA ** -0.5),
        'a_a0': nrm((L, A_WIDTH), 0.1),
        'a_a2': nrm((L, A_ICLR_LORA, A_WIDTH), 0.1 * A_ICLR_LORA ** -0.5),
        'a_g2': nrm((L, A_GATE_LORA, A_WIDTH), A_GATE_LORA ** -0.5),
        'a_kk': 0.85 + nrm((L, A_HEADS, A_HEAD_DIM), 0.02),
        'a_ka': 1.0 + nrm((L, A_HEADS, A_HEAD_DIM), 0.02),
        'a_rk': nrm((L, A_HEADS, A_HEAD_DIM), 0.1),
        'a_lnx_g': 1.0 + nrm((L, A_HEADS, A_HEAD_DIM), 0.02),
        'a_lnx_b': nrm((L, A_HEADS, A_HEAD_DIM), 0.02),
        'c_lam_re': -0.5 + nrm((L, C_GROUPS, C_STATE), 0.01),
        'c_lam_im': lam_im0 + nrm((L, C_GROUPS, C_STATE), 0.01),
        'c_log_dt': jax.random.uniform(next(ks), (L, C_GROUPS), f32,
                                       math.log(C_DT_MIN), math.log(C_DT_MAX)),
        'c_b_re': nrm((L, C_GROUPS, C_STATE, C_GROUP), (2.0 * C_GROUP) ** -0.5),
        'c_b_im': nrm((L, C_GROUPS, C_STATE, C_GROUP), (2.0 * C_GROUP) ** -0.5),
        'c_c_re': nrm((L, C_GROUPS, C_GROUP, C_STATE), C_STATE ** -0.5),
        'c_c_im': nrm((L, C_GROUPS, C_GROUP, C_STATE), C_STATE ** -0.5),
        'c_d': nrm((L, C_GROUPS, C_GROUP), 1.0),
        'c_w_glu': nrm((L, C_WIDTH, C_WIDTH), C_WIDTH ** -0.5),
        'c_b_glu': nrm((L, C_WIDTH), 0.02),
        'w_branch': nrm((L, BRANCH_ROWS, D), 1.0) * branch_scale,
        'w_out': nrm((L, D, D), DN_BETA * D ** -0.5),
        'ln1_g': 1.0 + nrm((L, D), 0.02),
        'ln1_b': nrm((L, D), 0.02),
        'w_ff1': nrm((L, D, D_FF), D ** -0.5),
        'w_ff2': nrm((L, D_FF, D), DN_BETA * D_FF ** -0.5),
        'ln2_g': 1.0 + nrm((L, D), 0.02),
        'ln2_b': nrm((L, D), 0.02),
    }


def reference(x, w_in, b_gate, a_shift, a_w0, a_w2, a_a0, a_a2, a_g2, a_kk, a_ka, a_rk,
              a_lnx_g, a_lnx_b, c_lam_re, c_lam_im, c_log_dt, c_b_re, c_b_im, c_c_re,
              c_c_im, c_d, c_w_glu, c_b_glu, w_branch, w_out, ln1_g, ln1_b, w_ff1, w_ff2,
              ln2_g, ln2_b):
    for l in range(DEPTH):
        proj = x @ w_in[l]
        z_a, q_b, k_b, v_b, g_b, u_c, gate_logits = jnp.split(proj, PROJ_SPLITS, axis=-1)
        z_a = _token_shift(z_a, a_shift[l])
        y_a = _rwkv7_mixer(z_a, a_w0[l], a_w2[l], a_a0[l], a_a2[l], a_g2[l],
                           a_kk[l], a_ka[l], a_rk[l], a_lnx_g[l], a_lnx_b[l])
        y_b = _retention_mixer(q_b, k_b, v_b, g_b)
        y_c = _s5_mixer(u_c, c_lam_re[l], c_lam_im[l], c_log_dt[l], c_b_re[l], c_b_im[l],
                        c_c_re[l], c_c_im[l], c_d[l], c_w_glu[l], c_b_glu[l])
        wb_a, wb_b, wb_c = jnp.split(w_branch[l], (A_WIDTH, A_WIDTH + B_V_WIDTH), axis=0)
        gate_a, gate_b, gate_c = jnp.split(jax.nn.sigmoid(gate_logits + b_gate[l]), 3, axis=-1)
        merged = gate_a * (y_a @ wb_a) + gate_b * (y_b @ wb_b) + gate_c * (y_c @ wb_c)
        x = _layer_norm(DN_ALPHA * x + merged @ w_out[l], ln1_g[l], ln1_b[l])
        ff = jnp.square(jax.nn.relu(x @ w_ff1[l])) @ w_ff2[l]
        x = _layer_norm(DN_ALPHA * x + ff, ln2_g[l], ln2_b[l])
    return x
```

```python
import contextlib
import os
import math
import numpy as np
from concourse.bass_utils import run_bass_kernel_spmd
import concourse.bass as bass
import concourse.mybir as mybir

F32 = mybir.dt.float32
BF16 = mybir.dt.bfloat16
AF = mybir.ActivationFunctionType
ALU = mybir.AluOpType
AX = mybir.AxisListType


class Sem:
    def __init__(self, h, name):
        self.h = h
        self.name = name
        self.count = 0


class T:
    def __init__(self, fw, h, name):
        self.fw = fw
        self.h = h
        self.name = name
        self.tok_w = {}
        self.tok_r = {}
        self.dsem = None

    def __getitem__(self, idx):
        return V(self.h[idx], self)

    def ap(self):
        return V(self.h.ap() if hasattr(self.h, "ap") and callable(getattr(self.h, "ap")) else self.h[:], self)


class V:
    def __init__(self, ap, t):
        self.ap = ap
        self.t = t

    def __getitem__(self, idx):
        return V(self.ap[idx], self.t)

    def rearrange(self, s, **kw):
        return V(self.ap.rearrange(s, **kw), self.t)

    def bitcast(self, dt):
        return V(self.ap.bitcast(dt), self.t)

    def to_broadcast(self, shape):
        return V(self.ap.to_broadcast(shape), self.t)

    def partition_broadcast(self, n):
        return V(self.ap.partition_broadcast(n), self.t)


class EngState:
    def __init__(self, name):
        self.name = name
        self.sem = None
        self.n = 0
        self.seen = {}
        self.ops = []


class FW:
    ENG = ("pe", "act", "dve", "pool", "sp")

    def __init__(self, nc, stack):
        self.nc = nc
        self.stack = stack
        self.e = {n: EngState(n) for n in self.ENG}
        for n in self.ENG:
            self.e[n].sem = self.new_sem("e_" + n)
        self.const_sem = self.new_sem("consts")
        self.const_ts = []
        self.nsb = 0

    def new_sem(self, name):
        h = self.stack.enter_context(self.nc.semaphore(name))
        sm = Sem(h, name)
        if not hasattr(self, "all_sems"):
            self.all_sems = []
        self.all_sems.append(sm)
        return sm

    def sb(self, name, shape, dt=F32):
        name = "s_" + name
        h = self.stack.enter_context(self.nc.sbuf_tensor(name, list(shape), dt))
        return T(self, h, name)

    def ps(self, name, shape, dt=F32):
        name = "p_" + name
        h = self.stack.enter_context(self.nc.psum_tensor(name, list(shape), dt))
        return T(self, h, name)

    def dram(self, name, shape, dt, kind="Internal"):
        h = self.nc.dram_tensor(name, list(shape), dt, kind=kind)
        t = T(self, h, name)
        t.is_dram = True
        return t

    def op(self, eng, fn, reads=(), writes=(), dsem=None, extra=()):
        E = self.e[eng]
        deps = {}
        for s_, val_ in extra:
            deps[s_] = max(deps.get(s_, 0), val_)
        for v in reads:
            for s, val in v.t.tok_w.items():
                deps[s] = max(deps.get(s, 0), val)
        for v in writes:
            for s, val in v.t.tok_w.items():
                deps[s] = max(deps.get(s, 0), val)
            for s, val in v.t.tok_r.items():
                deps[s] = max(deps.get(s, 0), val)
        waits = []
        for s, val in deps.items():
            if s is E.sem and eng == "pe":
                continue
            if E.seen.get(s, 0) >= val:
                continue
            E.seen[s] = val
            waits.append((s, val))
        if dsem is not None:
            dsem.count += 16
            tok = (dsem, dsem.count)
            inc = (dsem, 16)
        else:
            E.n += 1
            tok = (E.sem, E.n)
            inc = (E.sem, 1)
        wset = set(id(v.t) for v in writes)
        for v in writes:
            v.t.tok_w = {tok[0]: tok[1]}
            v.t.tok_r = {}
        for v in reads:
            if id(v.t) not in wset:
                v.t.tok_r[tok[0]] = max(v.t.tok_r.get(tok[0], 0), tok[1])
        E.ops.append((waits, fn, inc))
        return tok

    def emit(self):
        nc = self.nc
        fw = self

        def run(engobj, E):
            for waits, fn, inc in E.ops:
                for s, val in waits:
                    engobj.wait_ge(s.h, val)
                if fn is None:
                    continue
                ins = fn(engobj)
                ins.then_inc(inc[0].h, inc[1])

        with nc.Block() as block:
            @block.tensor
            def _(t):
                run(t, fw.e["pe"])

            @block.scalar
            def _(t):
                run(t, fw.e["act"])

            @block.vector
            def _(t):
                run(t, fw.e["dve"])

            @block.gpsimd
            def _(t):
                run(t, fw.e["pool"])

            @block.sync
            def _(t):
                run(t, fw.e["sp"])

    def mm(self, out, lhsT, rhs, start=True, stop=True):
        return self.op("pe", lambda e: e.matmul(out.ap, lhsT=lhsT.ap, rhs=rhs.ap, start=start, stop=stop),
                       reads=[lhsT, rhs], writes=[out])

    def tr(self, out, in_, ident):
        return self.op("pe", lambda e: e.transpose(out.ap, in_.ap, ident.ap), reads=[in_, ident], writes=[out])

    def act(self, out, in_, func, bias=None, scale=1.0, eng="act", accum=None):
        reads = [in_]
        kw = {}
        if isinstance(bias, V):
            reads.append(bias)
            kw["bias"] = bias.ap
        elif bias is not None:
            kw["bias"] = bias
        if isinstance(scale, V):
            reads.append(scale)
            kw["scale"] = scale.ap
        else:
            kw["scale"] = scale
        writes = [out]
        if accum is not None:
            kw["accum_out"] = accum.ap
            writes.append(accum)
        return self.op(eng, lambda e: e.activation(out.ap, in_.ap, func, **kw), reads=reads, writes=writes)

    def tt(self, out, in0, in1, op, eng="dve", extra=()):
        return self.op(eng, lambda e: e.tensor_tensor(out.ap, in0.ap, in1.ap, op), reads=[in0, in1], writes=[out],
                       extra=extra)

    def ts(self, out, in0, s1, s2=None, op0=ALU.mult, op1=None, eng="dve", accum=None):
        if eng == "pool":
            eng = "dve"
        reads = [in0]
        a1 = s1
        a2 = s2
        if isinstance(s1, V):
            reads.append(s1)
            a1 = s1.ap
        if isinstance(s2, V):
            reads.append(s2)
            a2 = s2.ap
        kw = {}
        if op1 is not None:
            kw["op1"] = op1
        writes = [out]
        if accum is not None:
            kw["accum_out"] = accum.ap
            writes.append(accum)
        return self.op(eng, lambda e: e.tensor_scalar(out.ap, in0.ap, a1, a2, op0, **kw), reads=reads, writes=writes)

    def stt(self, out, in0, scalar, in1, op0, op1, eng="dve"):
        if eng == "pool":
            eng = "dve"
        reads = [in0, in1]
        a = scalar
        if isinstance(scalar, V):
            reads.append(scalar)
            a = scalar.ap
        return self.op(eng, lambda e: e.scalar_tensor_tensor(out.ap, in0.ap, a, in1.ap, op0, op1),
                       reads=reads, writes=[out])

    def copy(self, out, in_, eng="dve"):
        if eng == "pool":
            eng = "dve"
        if eng == "act":
            return self.op(eng, lambda e: e.copy(out.ap, in_.ap), reads=[in_], writes=[out])
        return self.op(eng, lambda e: e.tensor_copy(out.ap, in_.ap), reads=[in_], writes=[out])

    def memset(self, out, val, eng="dve"):
        eng = "dve"
        return self.op(eng, lambda e: e.memset(out.ap, val), reads=[], writes=[out])

    def dma(self, out, in_, eng="sp", sem_on=None, extra=(), **kw):
        t = sem_on if sem_on is not None else out.t
        if t.dsem is None:
            t.dsem = self.new_sem("d_" + t.name)
        return self.op(eng, lambda e: e.dma_start(out=out.ap, in_=in_.ap, **kw), reads=[in_], writes=[out],
                       dsem=t.dsem, extra=extra)

    def const_dma(self, out, in_, eng="sp", **kw):
        self.const_ts.append(out.t)
        return self.op(eng, lambda e: e.dma_start(out=out.ap, in_=in_.ap, **kw), reads=[in_], writes=[out],
                       dsem=self.const_sem)

    def finish_consts(self):
        for t in self.const_ts:
            t.tok_w = {self.const_sem: self.const_sem.count}

    def final_wait(self, toks, eng="sp"):
        allt = {}
        for s, v in toks:
            allt[s] = max(allt.get(s, 0), v)
        for s in self.all_sems:
            if s.count > 0 and not s.name.startswith("e_"):
                allt[s] = max(allt.get(s, 0), s.count)
        self.e[eng].ops.append(([(s, v) for s, v in allt.items()], None, None))

    def barrier(self, extra=()):
        comp = ("pe", "act", "dve", "pool")
        for n in comp + ("sp",):
            E = self.e[n]
            waits = []
            for m in comp:
                if m == n:
                    continue
                E2 = self.e[m]
                if E2.n > 0 and E.seen.get(E2.sem, 0) < E2.n:
                    E.seen[E2.sem] = E2.n
                    waits.append((E2.sem, E2.n))
            for (s, v) in extra:
                if E.seen.get(s, 0) < v:
                    E.seen[s] = v
                    waits.append((s, v))
            if waits:
                E.ops.append((waits, None, None))

import math

D = 1024
SEQ = 2048
TT = 512
NCH = 4
L = 2
AW = 512
ALPHA = (2.0 * 2) ** 0.25
LN_EPS = 1e-5
A_GN_EPS = 64e-5
B_GN_EPS = 1e-5
O_R, O_K, O_V, O_ZW, O_ZA, O_ZG = 0, 512, 1024, 1536, 1600, 1664
O_BQ, O_BK, O_BV, O_BG, O_CU, O_GATE = 1792, 2304, 2816, 3840, 4864, 5376
PC_MU_R, PC_MU_K, PC_MU_V, PC_MU_ZW, PC_MU_ZA, PC_MU_ZG = 0, 8, 16, 24, 25, 26
PC_KK, PC_KA, PC_RK, PC_W0, PC_A0 = 27, 35, 43, 51, 59
PC_BG, PC_CD, PC_CBG = 67, 91, 95
PC_LRE, PC_LIM, PC_LDT = 99, 115, 131
PC_LN = 147
NPC = 179
GAMMAS = [1.0 - 2.0 ** (-5.0 - h) for h in range(4)]
S5L = 64


def host_consts():
    f32 = np.float32
    c = {}
    c["ident"] = np.eye(128, dtype=f32)
    s = np.arange(128)[:, None]
    t = np.arange(128)[None, :]
    m = np.zeros((128, 3, 128), f32)
    m[:, 0, :] = (s < t)
    m[:, 1, :] = (s <= t)
    m[:, 2, :] = (s > t)
    c["masks"] = m
    half = 64
    inv_freq = (10000.0 ** (-np.arange(half, dtype=f32) / half)).astype(f32)
    pos = np.arange(SEQ, dtype=f32)
    ang = (pos[:, None] * inv_freq[None, :]).astype(f32)
    cos = np.cos(ang.astype(np.float64)).astype(f32).T
    sin = np.sin(ang.astype(np.float64)).astype(f32).T
    rope = np.zeros((128, 2, SEQ), f32)
    rope[:64, 0] = cos
    rope[64:, 0] = cos
    rope[:64, 1] = -sin
    rope[64:, 1] = sin
    c["rope"] = rope
    ps = np.zeros((128, 128), f32)
    for d in range(128):
        ps[(d + 64) % 128, d] = 1.0
    c["pswap"] = ps
    dm = np.zeros((128, 4, 128), f32)
    rc = np.zeros((128, 8), f32)
    i = np.arange(128, dtype=np.float64)
    for h in range(4):
        lg = math.log(GAMMAS[h])
        rel = t - s
        dm[:, h, :] = np.where(rel >= 0, np.exp(lg * np.maximum(rel, 0)), 0.0) * (128 ** -0.5)
        rc[:, h] = np.exp(lg * (i + 1.0))
        rc[:, 4 + h] = np.exp(lg * (127.0 - i)) * (128 ** -0.5)
    c["dmask"] = dm
    c["retcols"] = rc
    c["jrow"] = np.tile(np.arange(1, S5L + 1, dtype=f32)[None, :], (128, 1))
    sm = np.zeros((128, 4), f32)
    for p in range(128):
        g8 = p // 16
        for e in range(2):
            for g2 in range(2):
                if g8 % 4 == 2 * e + g2:
                    sm[p, e * 2 + g2] = 1.0
    c["s5mask"] = sm
    return c


def host_params(inp):
    f32 = np.float32
    pcol = np.zeros((L, 128, NPC), f32)
    prow = np.zeros((L, 1, 1024), f32)
    s5b = np.zeros((L, 5, 128, 4, 64), f32)
    s5c = np.zeros((L, 2, 128, 16, 16), f32)
    for l in range(L):
        sh = inp["a_shift"][l]
        for h in range(8):
            pcol[l, :64, PC_MU_R + h] = sh[O_R + 64 * h:O_R + 64 * h + 64]
            pcol[l, :64, PC_MU_K + h] = sh[O_K + 64 * h:O_K + 64 * h + 64]
            pcol[l, :64, PC_MU_V + h] = sh[O_V + 64 * h:O_V + 64 * h + 64]
            pcol[l, :64, PC_KK + h] = inp["a_kk"][l, h]
            pcol[l, :64, PC_KA + h] = inp["a_ka"][l, h]
            pcol[l, :64, PC_RK + h] = inp["a_rk"][l, h]
            pcol[l, :64, PC_W0 + h] = inp["a_w0"][l, 64 * h:64 * h + 64]
            pcol[l, :64, PC_A0 + h] = inp["a_a0"][l, 64 * h:64 * h + 64]
        pcol[l, :64, PC_MU_ZW] = sh[O_ZW:O_ZW + 64]
        pcol[l, :64, PC_MU_ZA] = sh[O_ZA:O_ZA + 64]
        pcol[l, :, PC_MU_ZG] = sh[O_ZG:O_ZG + 128]
        pcol[l, :, PC_BG:PC_BG + 24] = inp["b_gate"][l].reshape(24, 128).T
        pcol[l, :, PC_CD:PC_CD + 4] = inp["c_d"][l].reshape(4, 128).T
        pcol[l, :, PC_CBG:PC_CBG + 4] = inp["c_b_glu"][l].reshape(4, 128).T
        lre = inp["c_lam_re"][l]
        lim = inp["c_lam_im"][l]
        ldt = inp["c_log_dt"][l]
        for gp in range(16):
            for g2 in range(2):
                g = 2 * gp + g2
                pcol[l, 64 * g2:64 * g2 + 64, PC_LRE + gp] = lre[g]
                pcol[l, 64 * g2:64 * g2 + 64, PC_LIM + gp] = lim[g]
                pcol[l, 64 * g2:64 * g2 + 64, PC_LDT + gp] = ldt[g]
        pcol[l, :, PC_LN:PC_LN + 8] = inp["ln1_g"][l].reshape(8, 128).T
        pcol[l, :, PC_LN + 8:PC_LN + 16] = inp["ln1_b"][l].reshape(8, 128).T
        pcol[l, :, PC_LN + 16:PC_LN + 24] = inp["ln2_g"][l].reshape(8, 128).T
        pcol[l, :, PC_LN + 24:PC_LN + 32] = inp["ln2_b"][l].reshape(8, 128).T
        prow[l, 0, :512] = inp["a_lnx_g"][l].reshape(512)
        prow[l, 0, 512:] = inp["a_lnx_b"][l].reshape(512)
        bre = inp["c_b_re"][l]
        bim = inp["c_b_im"][l]
        for kc in range(4):
            for g8 in range(8):
                g = 8 * kc + g8
                s5b[l, 0, 16 * g8:16 * g8 + 16, kc, :] = lre[g][None, :]
                s5b[l, 1, 16 * g8:16 * g8 + 16, kc, :] = lim[g][None, :]
                s5b[l, 2, 16 * g8:16 * g8 + 16, kc, :] = ldt[g]
                s5b[l, 3, 16 * g8:16 * g8 + 16, kc, :] = bre[g].T
                s5b[l, 4, 16 * g8:16 * g8 + 16, kc, :] = bim[g].T
        cre = inp["c_c_re"][l]
        cim = inp["c_c_im"][l]
        for gp in range(16):
            for g2 in range(2):
                g = 2 * gp + g2
                s5c[l, 0, 64 * g2:64 * g2 + 64, gp, :] = cre[g].T
                s5c[l, 1, 64 * g2:64 * g2 + 64, gp, :] = cim[g].T
    return pcol, prow, s5b, s5c


class Prog:
    def __init__(self, nseq=2, ntile=4, nlayer=2, dbg=None, stages="ABCMF"):
        self.nseq, self.ntile, self.nlayer = nseq, ntile, nlayer
        self.dbg = dbg or []
        self.stages = stages
        self.nc = bass.Bass("TRN2", target_bir_lowering=False)
        self.dbg_out = {}

    def phase(self, name, extra=()):
        self.fw.barrier(extra)
        self.cur_phase = name
        self.aoff = 0

    def A(self, name, shape, dt=F32):
        key = (self.cur_phase, name)
        words = 1
        for s in shape[1:]:
            words *= s
        if dt == BF16:
            words = (words + 1) // 2
        if key in self.acache:
            t, off, w = self.acache[key]
            assert w == words, (key, w, words)
            return t
        assert self.aoff + words <= self.AWORDS, ("arena overflow", key, self.aoff, words)
        ap = self.arena.h[0:shape[0], self.aoff:self.aoff + words]
        if dt != F32:
            ap = ap.bitcast(dt)
        if len(shape) == 3:
            ap = ap.rearrange("p (a b) -> p a b", a=shape[1])
        elif len(shape) == 4:
            ap = ap.rearrange("p (a b c) -> p a b c", a=shape[1], b=shape[2])
        t = T(self.fw, ap, "ar_%s_%s" % key)
        self.acache[key] = (t, self.aoff, words)
        self.aoff += words
        return t

    def dump(self, name, v, shape, dt=F32):
        if name not in self.dbg:
            return
        if name in self.dbg_out:
            return
        d = self.fw.dram("dbg_" + name, list(shape), dt, kind="ExternalOutput")
        self.dbg_out[name] = d
        tok = self.fw.dma(V(d.h.ap(), d), v, sem_on=d)
        self.final_toks.append(tok)
        self.fw.barrier([tok])

    def build(self):
        nc = self.nc
        stack = contextlib.ExitStack()
        with stack:
            fw = self.fw = FW(nc, stack)
            self.final_toks = []
            self.acache = {}
            self.setup()
            stop = int(os.environ.get("KD_STOP", "0"))
            for l in range(self.nlayer if stop != 1 else 0):
                self.layer_setup(l)
                if stop == 2:
                    break
                for s in range(self.nseq):
                    self.seq_init(s)
                    if stop == 3:
                        break
                    for ti in range(self.ntile):
                        if stop == 4:
                            self.stage_x0(s, ti, l)
                            break
                        self.body(s, ti, l)
            fw.final_wait(self.final_toks)
            fw.emit()
        return nc

    def setup(self):
        fw = self.fw
        nl = self.nlayer
        dr = lambda n, sh, dt=F32: fw.dram(n, sh, dt, kind="ExternalInput")
        self.x_d = dr("x", [self.nseq, SEQ, D])
        self.y_d = fw.dram("y", [self.nseq, SEQ, D], F32, kind="ExternalOutput")
        sm = bool(os.environ.get("KD_SMALLW"))
        self.w_in_d = dr("w_in", [L, D, 8448] if not sm else [L, 8, 8])
        self.w_br_d = dr("w_branch", [L, 2048, D] if not sm else [L, 8, 8])
        self.w_out_d = dr("w_out", [L, D, D] if not sm else [L, 8, 8])
        self.w_ff1_d = dr("w_ff1", [L, D, 4096] if not sm else [L, 8, 8])
        self.w_ff2_d = dr("w_ff2", [L, 4096, D] if not sm else [L, 8, 8])
        a_w2_d = dr("a_w2", [L, 64, 512])
        a_a2_d = dr("a_a2", [L, 64, 512])
        a_g2_d = dr("a_g2", [L, 128, 512])
        c_wglu_d = dr("c_w_glu", [L, 512, 512])
        pcol_d = dr("pcol", [L, 128, NPC])
        prow_d = dr("prow", [L, 1, 1024])
        s5b_d = dr("s5b", [L, 5, 128, 4, 64])
        s5c_d = dr("s5c", [L, 2, 128, 16, 16])
        ident_d = dr("ident", [128, 128])
        masks_d = dr("masks", [128, 3, 128])
        self.rope_d = dr("rope", [128, 2, SEQ])
        pswap_d = dr("pswap", [128, 128])
        dmask_d = dr("dmask", [128, 4, 128])
        retcols_d = dr("retcols", [128, 8])
        jrow_d = dr("jrow", [128, S5L])
        s5mask_d = dr("s5mask", [128, 4])
        self.w_in_s = [fw.dram("w_in_s%d" % l, [D, 8448], BF16) for l in range(nl)]
        self.w_br_s = [fw.dram("w_br_s%d" % l, [2048, D], BF16) for l in range(nl)]
        self.w_out_s = [fw.dram("w_out_s%d" % l, [D, D], BF16) for l in range(nl)]
        self.w_ff1_s = [fw.dram("w_ff1_s%d" % l, [D, 4096], BF16) for l in range(nl)]
        self.w_ff2_s = [fw.dram("w_ff2_s%d" % l, [4096, D], BF16) for l in range(nl)]
        self.conv_toks = []
        sb = fw.sb
        self.ident = sb("ident", [128, 128])
        self.identb = sb("identb", [128, 128], BF16)
        masks_f = sb("masks_f", [128, 3, 128])
        self.mask4 = sb("mask4", [128, 512])
        self.mlow = sb("mlow", [128, 128])
        self.pswap = sb("pswap", [128, 128], BF16)
        pswap_f = sb("pswap_f", [128, 128])
        self.dmask = sb("dmask", [128, 4, 128])
        self.retcols = sb("retcols", [128, 8])
        self.jrow = sb("jrow", [128, S5L])
        self.s5mask = sb("s5mask", [128, 4])
        self.ones_f = sb("ones_f", [128, 128])
        self.ones_b = sb("ones_b", [128, 128], BF16)
        self.pcol = [sb("pcol", [128, NPC])] * nl
        self.omu = [sb("omu", [128, 27])] * nl
        self.oka = [sb("oka", [128, 8])] * nl
        self.prow = [sb("prow", [128, 1024])] * nl
        self.w2b = [sb("w2b", [64, 512], BF16)] * nl
        self.a2b = [sb("a2b", [64, 512], BF16)] * nl
        self.g2b = [sb("g2b", [128, 512], BF16)] * nl
        self.wglu = [sb("wglu", [128, 4, 512], BF16)] * nl
        s5b = self.s5b_t = [sb("s5b", [128, 5, 4, 64])] * nl
        s5c = self.s5c_t = [sb("s5c", [128, 2, 16, 16])] * nl
        self.s5cos = [sb("s5cos", [128, 16, S5L])] * nl
        self.s5sin = [sb("s5sin", [128, 16, S5L])] * nl
        self.s5rho = [sb("s5rho", [128, 16])] * nl
        self.s5are = [sb("s5are", [128, 16])] * nl
        self.s5aim = [sb("s5aim", [128, 16])] * nl
        self.s5B = [sb("s5B", [128, 2, 4, 2, 128], BF16)] * nl
        self.s5C = [sb("s5C", [128, 2, 16, 128], BF16)] * nl
        self.rwS = [sb("rwS", [64, 8, 64])] * nl
        self.rwSb = [sb("rwSb", [64, 8, 64], BF16)] * nl
        self.rtR = [sb("rtR", [128, 4, 256])] * nl
        self.rtRb = [sb("rtRb", [128, 4, 256], BF16)] * nl
        self.s5st = [sb("s5st", [128, 2, 16])] * nl
        self.carry = [sb("carry", [128, 27])] * nl
        self.xf = sb("xf", [128, 8, TT])
        self.xb = sb("xb", [128, 8, TT], BF16)
        self.yaT = sb("yaT", [128, 4, TT], BF16)
        self.ybT = sb("ybT", [128, 8, TT], BF16)
        self.ycT = sb("ycT", [128, 4, TT], BF16)
        self.ropet = sb("ropet", [128, 2, TT])
        self.wring = [sb("wring%d" % i, [128, 8, 512], BF16) for i in range(3)]
        self.wri = 0
        self.AWORDS = 20 * 1024 + 512
        self.arena = sb("arena", [128, self.AWORDS])
        self.ps = [fw.ps("ps%d" % i, [128, 512]) for i in range(8)]
        cd = fw.const_dma
        cd(self.ident[:], V(ident_d.h.ap(), ident_d))
        cd(masks_f[:], V(masks_d.h.ap(), masks_d))
        cd(pswap_f[:], V(pswap_d.h.ap(), pswap_d))
        cd(self.dmask[:], V(dmask_d.h.ap(), dmask_d))
        cd(self.retcols[:], V(retcols_d.h.ap(), retcols_d))
        cd(self.jrow[:], V(jrow_d.h.ap(), jrow_d))
        cd(self.s5mask[:], V(s5mask_d.h.ap(), s5mask_d))
        fw.finish_consts()
        self.lay_d = (pcol_d, prow_d, s5b_d, s5c_d, a_w2_d, a_a2_d, a_g2_d, c_wglu_d)
        self.xmid = fw.dram("xmid", [self.nseq, SEQ, D], F32)
        fw.copy(self.identb[:], self.ident[:])
        fw.copy(self.pswap[:], pswap_f[:])
        fw.memset(self.ones_f[:], 1.0)
        fw.memset(self.ones_b[:], 1.0)
        for i in range(4):
            fw.copy(self.mask4[:, i * 128:(i + 1) * 128], masks_f[:, i % 2, :])
        fw.copy(self.mlow[:], masks_f[:, 2, :])
        self.cur_phase = "setup"
        self.aoff = 0
        if not os.environ.get("KD_NOCONV"):
            self.convert_layer(0)

    def layer_setup(self, l):
        fw = self.fw
        pcol_d, prow_d, s5b_d, s5c_d, a_w2_d, a_a2_d, a_g2_d, c_wglu_d = self.lay_d
        self.phase("laysetup")
        fw.dma(self.pcol[l][:], V(pcol_d.h.ap()[l], pcol_d))
        fw.dma(self.prow[l][:], V(prow_d.h.ap()[l].to_broadcast([128, 1024]), prow_d))
        fw.dma(self.s5b_t[l][:], V(s5b_d.h.ap()[l].rearrange("q p k c -> p q k c"), s5b_d))
        fw.dma(self.s5c_t[l][:], V(s5c_d.h.ap()[l].rearrange("q p g c -> p q g c"), s5c_d))
        pc = self.pcol[l]
        fw.ts(self.omu[l][:], pc[:, 0:27], -1.0, 1.0, op0=ALU.mult, op1=ALU.add)
        fw.ts(self.oka[l][:], pc[:, PC_KA:PC_KA + 8], -1.0, 1.0, op0=ALU.mult, op1=ALU.add)
        if not os.environ.get("KD_NOS5"):
            self.s5_setup(l, self.s5b_t[l], self.s5c_t[l])
        for dst_, src_v in ((self.w2b[l], V(a_w2_d.h.ap()[l], a_w2_d)), (self.a2b[l], V(a_a2_d.h.ap()[l], a_a2_d)),
                            (self.g2b[l], V(a_g2_d.h.ap()[l], a_g2_d)),
                            (self.wglu[l], V(c_wglu_d.h.ap()[l].rearrange("(kc p) n -> p kc n", p=128), c_wglu_d))):
            if os.environ.get("KD_NOLORA"):
                break
            tok = fw.dma(dst_[:], src_v, eng="pool", extra=self.conv_toks)
            self.conv_toks = [tok]

    def convert_layer(self, l):
        fw = self.fw
        prev = list(self.conv_toks)
        self.conv_toks = []
        first = True
        for (src, dst, R, C) in ((self.w_in_d, self.w_in_s[l], D, 8448), (self.w_br_d, self.w_br_s[l], 2048, D),
                                 (self.w_out_d, self.w_out_s[l], D, D), (self.w_ff1_d, self.w_ff1_s[l], D, 4096),
                                 (self.w_ff2_d, self.w_ff2_s[l], 4096, D)):
            if C == 8448:
                sv = src.h.ap()[l].rearrange("r (a c) -> (r a) c", c=1408)
                dv = dst.h.ap().rearrange("r (a c) -> (r a) c", c=1408)
            elif C == 4096:
                sv = src.h.ap()[l].rearrange("r (a c) -> (r a) c", c=1024)
                dv = dst.h.ap().rearrange("r (a c) -> (r a) c", c=1024)
            else:
                sv = src.h.ap()[l]
                dv = dst.h.ap()
            tok = fw.dma(V(dv, dst), V(sv, src), eng="pool", extra=prev)
            prev = [tok]
            self.conv_toks = [tok]

    def sincos(self, out_sin, out_cos, ang, shape):
        fw = self.fw
        sfx = "_".join(str(x) for x in shape)
        nf = self.A("sc_n" + sfx, shape)
        ni = self.A("sc_i" + sfx, shape, mybir.dt.int32)
        r = self.A("sc_r" + sfx, shape)
        full = tuple(slice(None) for _ in shape)
        for out, off in ((out_sin, 0.0), (out_cos, math.pi / 2)):
            fw.ts(nf[full], ang, off, 1.0 / (2 * math.pi), op0=ALU.add, op1=ALU.mult)
            fw.copy(ni[full], nf[full])
            fw.copy(nf[full], ni[full])
            fw.stt(r[full], nf[full], -2 * math.pi, ang, ALU.mult, ALU.add)
            if off != 0.0:
                fw.ts(r[full], r[full], off, None, op0=ALU.add)
            fw.ts(r[full], r[full], 3.1415925, -3.1415925, op0=ALU.min, op1=ALU.max)
            fw.act(out, r[full], AF.Sin)

    def s5_setup(self, l, s5b, s5c):
        fw = self.fw
        pc = self.pcol[l]
        self.phase("s5setup")
        dt = self.A("dtA", [128, 16])
        fw.act(dt[:], pc[:, PC_LDT:PC_LDT + 16], AF.Exp)
        lrd = self.A("lrd", [128, 16])
        lid = self.A("lid", [128, 16])
        fw.tt(lrd[:], pc[:, PC_LRE:PC_LRE + 16], dt[:], ALU.mult)
        fw.tt(lid[:], pc[:, PC_LIM:PC_LIM + 16], dt[:], ALU.mult)
        fw.act(self.s5rho[l][:], lrd[:], AF.Exp)
        sn = self.A("snA", [128, 16])
        cs = self.A("csA", [128, 16])
        self.sincos(sn[:], cs[:], lid[:], [128, 16])
        fw.tt(self.s5are[l][:], self.s5rho[l][:], cs[:], ALU.mult)
        fw.tt(self.s5aim[l][:], self.s5rho[l][:], sn[:], ALU.mult)
        ang = self.A("angT", [128, 16, S5L])
        for gp in range(16):
            fw.ts(ang[:, gp, :], self.jrow[:], lid[:, gp:gp + 1], None, op0=ALU.mult)
        self.sincos(self.s5sin[l][:], self.s5cos[l][:], ang[:], [128, 16, S5L])
        self.phase("s5setupB")
        sh = [128, 4, 64]
        dtb = self.A("dtb", sh)
        fw.act(dtb[:], s5b[:, 2], AF.Exp)
        lr = s5b[:, 0]
        li = s5b[:, 1]
        lrd2 = self.A("lrd2", sh)
        lid2 = self.A("lid2", sh)
        fw.tt(lrd2[:], lr, dtb[:], ALU.mult)
        fw.tt(lid2[:], li, dtb[:], ALU.mult)
        mag = self.A("mag", sh)
        fw.act(mag[:], lrd2[:], AF.Exp)
        sn2 = self.A("sn2", sh)
        cs2 = self.A("cs2", sh)
        self.sincos(sn2[:], cs2[:], lid2[:], sh)
        abr = self.A("abr", sh)
        abi = self.A("abi", sh)
        fw.tt(abr[:], mag[:], cs2[:], ALU.mult)
        fw.ts(abr[:], abr[:], -1.0, None, op0=ALU.add)
        fw.tt(abi[:], mag[:], sn2[:], ALU.mult)
        den = self.A("den", sh)
        t1 = self.A("t1", sh)
        fw.tt(den[:], lr, lr, ALU.mult)
        fw.tt(t1[:], li, li, ALU.mult)
        fw.tt(den[:], den[:], t1[:], ALU.add)
        fw.op("dve", lambda e: e.reciprocal(den.h[:], den.h[:]), reads=[den[:]], writes=[den[:]])
        fre = self.A("fre", sh)
        fim = self.A("fim", sh)
        fw.tt(fre[:], abr[:], lr, ALU.mult)
        fw.tt(t1[:], abi[:], li, ALU.mult)
        fw.tt(fre[:], fre[:], t1[:], ALU.add)
        fw.tt(fre[:], fre[:], den[:], ALU.mult)
        fw.tt(fim[:], abi[:], lr, ALU.mult)
        fw.tt(t1[:], abr[:], li, ALU.mult)
        fw.tt(fim[:], fim[:], t1[:], ALU.subtract)
        fw.tt(fim[:], fim[:], den[:], ALU.mult)
        bre = s5b[:, 3]
        bim = s5b[:, 4]
        bbr = self.A("bbr", sh)
        bbi = self.A("bbi", sh)
        fw.tt(bbr[:], fre[:], bre, ALU.mult)
        fw.tt(t1[:], fim[:], bim, ALU.mult)
        fw.tt(bbr[:], bbr[:], t1[:], ALU.subtract)
        fw.tt(bbi[:], fre[:], bim, ALU.mult)
        fw.tt(t1[:], fim[:], bre, ALU.mult)
        fw.tt(bbi[:], bbi[:], t1[:], ALU.add)
        for ri, bb in ((0, bbr), (1, bbi)):
            for kc in range(4):
                for e in range(2):
                    for g2 in range(2):
                        fw.ts(self.s5B[l][:, ri, kc, e, g2 * 64:(g2 + 1) * 64], bb[:, kc, :],
                              self.s5mask[:, e * 2 + g2:e * 2 + g2 + 1], None, op0=ALU.mult)
        fw.memset(self.s5C[l][:], 0.0)
        for gp in range(16):
            q = gp % 4
            for g2 in range(2):
                c0 = 32 * q + 16 * g2
                fw.copy(self.s5C[l][64 * g2:64 * g2 + 64, 0, gp, c0:c0 + 16], s5c[64 * g2:64 * g2 + 64, 0, gp, :])
                fw.ts(self.s5C[l][64 * g2:64 * g2 + 64, 1, gp, c0:c0 + 16], s5c[64 * g2:64 * g2 + 64, 1, gp, :],
                      -1.0, None, op0=ALU.mult)

    def seq_init(self, s):
        fw = self.fw
        if "A" not in self.stages:
            fw.memset(self.yaT[:], 0.0)
        if "B" not in self.stages:
            fw.memset(self.ybT[:], 0.0)
        if "C" not in self.stages:
            fw.memset(self.ycT[:], 0.0)
        for l in range(self.nlayer):
            if l > 0:
                break
            fw.memset(self.rwS[l][:], 0.0)
            fw.memset(self.rwSb[l][:], 0.0)
            fw.memset(self.rtR[l][:], 0.0)
            fw.memset(self.rtRb[l][:], 0.0)
            fw.memset(self.s5st[l][:], 0.0)
            fw.memset(self.carry[l][:], 0.0)

    def slot(self):
        w = self.wring[self.wri % 3]
        self.wri += 1
        return w

    def body(self, s, ti, l):
        self.stage_x0(s, ti, l)
        if "A" in self.stages:
            self.stage_A(s, ti, l)
        if "B" in self.stages:
            self.stage_B(s, ti, l)
        if "C" in self.stages:
            self.stage_C(s, ti, l)
        if "M" in self.stages:
            self.stage_M(s, ti, l)
        if "F" in self.stages:
            self.stage_F(s, ti, l)
        self.stage_out(s, ti, l)

    def stage_x0(self, s, ti, l):
        fw = self.fw
        self.phase("x0")
        xtok = self.A("xtok", [128, 4, D])
        src = self.x_d if l == 0 else self.xmid
        fw.dma(xtok[:], V(src.h.ap()[s, ti * TT:(ti + 1) * TT, :].rearrange("(b p) f -> p b f", p=128), src))
        k = 0
        for b in range(4):
            for half in range(2):
                p = self.ps[k % 2]
                k += 1
                for i in range(4):
                    fc = half * 4 + i
                    fw.tr(p[:, i * 128:(i + 1) * 128], xtok[:, b, fc * 128:(fc + 1) * 128], self.ident[:])
                pv = p[:].rearrange("p (i t) -> p i t", i=4)
                fw.copy(self.xf[:, half * 4:half * 4 + 4, b * 128:(b + 1) * 128], pv, eng="act")
                fw.copy(self.xb[:, half * 4:half * 4 + 4, b * 128:(b + 1) * 128],
                        self.xf[:, half * 4:half * 4 + 4, b * 128:(b + 1) * 128], eng="dve")

    def stage_out(self, s, ti, l):
        fw = self.fw
        self.phase("out")
        ytok = self.A("ytok", [128, 4, D])
        k = 0
        for b in range(4):
            for half in range(2):
                p = self.ps[k % 2]
                k += 1
                for i in range(4):
                    fc = half * 4 + i
                    fw.tr(p[:, i * 128:(i + 1) * 128], self.xf[:, fc, b * 128:(b + 1) * 128], self.ident[:])
                fw.copy(ytok[:, b, half * 512:(half + 1) * 512], p[:], eng="act" if k % 2 else "dve")
        dst = self.y_d if l == self.nlayer - 1 else self.xmid
        tok = fw.dma(V(dst.h.ap()[s, ti * TT:(ti + 1) * TT, :].rearrange("(b p) f -> p b f", p=128), dst),
                     ytok[:], sem_on=dst)
        self.final_toks.append(tok)
        last = (l == self.nlayer - 1 and s == self.nseq - 1 and ti == self.ntile - 1)
        if not last:
            self.fw.barrier([tok])

    def load_w(self, ws, src_t, src_ap):
        return self.fw.dma(ws, V(src_ap, src_t))

    def proj_fm(self, ps_v, w_v_fn, M, rhs_t=None):
        fw = self.fw
        for kc in range(8):
            fw.mm(ps_v, w_v_fn(kc), self.xb[:, kc, :], start=(kc == 0), stop=(kc == 7))

    def layer_norm(self, yv, l, gcol, bcol, name):
        fw = self.fw
        sq = self.A(name + "_sq", [128, 8, TT])
        for kc in range(8):
            fw.tt(sq[:, kc, :], yv[:, kc, :], yv[:, kc, :], ALU.mult, eng="pool" if kc % 2 else "dve")
        p1, p2 = self.ps[4], self.ps[5]
        for kc in range(8):
            fw.mm(p1[:], self.ones_f[:], yv[:, kc, :], start=(kc == 0), stop=(kc == 7))
        for kc in range(8):
            fw.mm(p2[:], self.ones_f[:], sq[:, kc, :], start=(kc == 0), stop=(kc == 7))
        mean = self.A(name + "_mean", [128, TT])
        rstd = self.A(name + "_rstd", [128, TT])
        msq = self.A(name + "_msq", [128, TT])
        fw.ts(mean[:], p1[:], 1.0 / D, None, op0=ALU.mult)
        fw.tt(msq[:], mean[:], mean[:], ALU.mult)
        fw.stt(rstd[:], p2[:], 1.0 / D, msq[:], ALU.mult, ALU.subtract)
        fw.ts(rstd[:], rstd[:], LN_EPS, None, op0=ALU.add)
        fw.act(rstd[:], rstd[:], AF.Sqrt)
        fw.op("dve", lambda e: e.reciprocal(rstd.h[:], rstd.h[:]), reads=[rstd[:]], writes=[rstd[:]])
        fw.tt(msq[:], mean[:], rstd[:], ALU.mult)
        for kc in range(8):
            e1 = "pool" if kc % 2 else "dve"
            fw.tt(yv[:, kc, :], yv[:, kc, :], rstd[:], ALU.mult, eng=e1)
            fw.tt(yv[:, kc, :], yv[:, kc, :], msq[:], ALU.subtract, eng=e1)
            fw.ts(self.xf[:, kc, :], yv[:, kc, :], gcol(kc), bcol(kc), op0=ALU.mult, op1=ALU.add)
            fw.copy(self.xb[:, kc, :], self.xf[:, kc, :], eng="act")

    def shift(self, ps_v, P, l, j, out_v):
        fw = self.fw
        psb = self.A("psb%d" % (self.shk % 2), [128, TT + 1])
        self.shk += 1
        pc = self.pcol[l]
        fw.copy(psb[0:P, 1:TT + 1], ps_v, eng="act")
        fw.copy(psb[0:P, 0:1], self.carry[l][0:P, j:j + 1], eng="pool")
        tmp = self.A("shtmp%d" % (self.shk % 2), [128, TT])
        fw.ts(tmp[0:P, :], psb[0:P, 0:TT], pc[0:P, j:j + 1], None, op0=ALU.mult, eng="pool")
        fw.stt(out_v, psb[0:P, 1:TT + 1], self.omu[l][0:P, j:j + 1], tmp[0:P, :], ALU.mult, ALU.add)
        fw.copy(self.carry[l][0:P, j:j + 1], psb[0:P, TT:TT + 1], eng="pool")

    def stage_A(self, s, ti, l):
        fw = self.fw
        self.phase("A")
        self.shk = 0
        pc = self.pcol[l]
        ps = self.ps
        C1 = 0.6065306597126334
        wl = self.slot()
        fw.dma(wl[:, :, 0:256], V(self.w_in_s[l].h.ap().rearrange("(kc p) n -> p kc n", p=128)[:, :, O_ZW:O_ZW + 256], self.w_in_s[l]))
        zw = self.A("zw", [64, TT])
        tzw = self.A("tzw", [64, TT], BF16)
        zab = self.A("zab", [64, TT], BF16)
        zg = self.A("zg", [128, TT])
        szg = self.A("szg", [128, TT], BF16)
        self.proj_fm(ps[0][0:64, :], lambda kc: wl[:, kc, 0:64], 64)
        self.shift(ps[0][0:64, :], 64, l, PC_MU_ZW, zw[:])
        fw.act(tzw[:], zw[:], AF.Tanh)
        self.proj_fm(ps[1][0:64, :], lambda kc: wl[:, kc, 64:128], 64)
        self.shift(ps[1][0:64, :], 64, l, PC_MU_ZA, zw[:])
        fw.copy(zab[:], zw[:], eng="act")
        self.proj_fm(ps[2][:, :], lambda kc: wl[:, kc, 128:256], 128)
        self.shift(ps[2][:, :], 128, l, PC_MU_ZG, zg[:])
        fw.act(szg[:], zg[:], AF.Sigmoid)
        O_sb = self.A("O_sb", [128, NCH, AW])
        vtok = self.A("vtok", [128, NCH, AW], BF16)
        rks = self.A("rks", [128, NCH, 8])
        ones_c = self.ones_f[0:64, 0:128]
        f = lambda nm, dt=F32: self.A(nm, [64, TT], dt)
        for hp in range(4):
            ws = self.slot()
            for q, off in enumerate((O_R, O_K, O_V)):
                fw.dma(ws[:, :, q * 128:(q + 1) * 128],
                       V(self.w_in_s[l].h.ap().rearrange("(kc p) n -> p kc n", p=128)[:, :, off + hp * 128:off + hp * 128 + 128], self.w_in_s[l]))
            for e in range(2):
                h = hp * 2 + e
                r = f("r"); k = f("k"); v = f("v")
                for q, (dst, pcb) in enumerate(((r, PC_MU_R), (k, PC_MU_K), (v, PC_MU_V))):
                    pp = ps[q]
                    self.proj_fm(pp[0:64, :], lambda kc, q=q: ws[:, kc, q * 128 + e * 64:q * 128 + e * 64 + 64], 64)
                    self.shift(pp[0:64, :], 64, l, pcb + h, dst[:])
                a = f("a"); sg = f("sg"); cs = f("cs")
                fw.mm(ps[3][0:64, :], self.a2b[l][:, h * 64:(h + 1) * 64], zab[:])
                fw.act(a[:], ps[3][0:64, :], AF.Sigmoid, bias=pc[0:64, PC_A0 + h:PC_A0 + h + 1])
                fw.mm(ps[3][0:64, :], self.w2b[l][:, h * 64:(h + 1) * 64], tzw[:])
                fw.act(sg[:], ps[3][0:64, :], AF.Sigmoid, bias=pc[0:64, PC_W0 + h:PC_W0 + h + 1])
                for c in range(NCH):
                    cl = slice(c * 128, (c + 1) * 128)
                    fw.op("dve", lambda en, cl=cl: en.tensor_tensor_scan(cs.h[:, cl], ones_c.ap, sg.h[:, cl], 0.0, ALU.mult, ALU.add),
                          reads=[sg[:], ones_c], writes=[cs[:]])
                eneg = f("eneg"); epos = f("epos"); epex = f("epex"); eend = f("eend"); tmp = f("tmp")
                fw.act(eneg[:], cs[:], AF.Exp, scale=C1)
                fw.act(epos[:], cs[:], AF.Exp, scale=-C1)
                fw.tt(tmp[:], cs[:], sg[:], ALU.subtract, eng="pool")
                fw.act(epex[:], tmp[:], AF.Exp, scale=-C1)
                nbc = self.A("nbc", [64, NCH])
                fw.ts(nbc[:], cs[:].rearrange("p (c t) -> p c t", t=128)[:, :, 127], -C1, None, op0=ALU.mult)
                for c in range(NCH):
                    cl = slice(c * 128, (c + 1) * 128)
                    fw.act(eend[:, cl], cs[:, cl], AF.Exp, scale=C1, bias=nbc[:, c:c + 1])
                kk = f("kk"); kkn = f("kkn")
                fw.ts(kk[:], k[:], pc[0:64, PC_KK + h:PC_KK + h + 1], None, op0=ALU.mult, eng="pool")
                fw.tt(tmp[:], kk[:], kk[:], ALU.mult, eng="pool")
                fw.mm(ps[3][0:64, :], self.ones_f[0:64, 0:64], tmp[:])
                fw.act(kkn[:], ps[3][0:64, :], AF.Sqrt)
                fw.ts(kkn[:], kkn[:], 1e-12, None, op0=ALU.max)
                fw.op("dve", lambda en: en.reciprocal(kkn.h[:], kkn.h[:]), reads=[kkn[:]], writes=[kkn[:]])
                fw.tt(kkn[:], kk[:], kkn[:], ALU.mult)
                kp = f("kp"); nb = f("nb")
                fw.ts(tmp[:], a[:], pc[0:64, PC_KA + h:PC_KA + h + 1], self.oka[l][0:64, h:h + 1], op0=ALU.mult, op1=ALU.add)
                fw.tt(kp[:], k[:], tmp[:], ALU.mult)
                fw.tt(nb[:], kkn[:], a[:], ALU.mult, eng="pool")
                nbt = f("nbt", BF16); kt = f("kt", BF16); nbe = f("nbe", BF16); ke = f("ke", BF16)
                vb = f("vb", BF16); prod = f("prod", BF16)
                nar = self.A("nar", [64, NCH, 2, 128], BF16)
                fw.tt(nbt[:], nb[:], eneg[:], ALU.mult)
                fw.tt(kt[:], kp[:], eneg[:], ALU.mult, eng="pool")
                fw.stt(nar[:, :, 0, :], kkn[:].rearrange("p (c t) -> p c t", t=128), -1.0,
                       epex[:].rearrange("p (c t) -> p c t", t=128), ALU.mult, ALU.mult)
                fw.tt(nar[:, :, 1, :], r[:].rearrange("p (c t) -> p c t", t=128),
                      epos[:].rearrange("p (c t) -> p c t", t=128), ALU.mult, eng="pool")
                fw.tt(nbe[:], nb[:], eend[:], ALU.mult)
                fw.tt(ke[:], kp[:], eend[:], ALU.mult, eng="pool")
                fw.copy(vb[:], v[:], eng="act")
                fw.stt(prod[:], r[:], pc[0:64, PC_RK + h:PC_RK + h + 1], kp[:], ALU.mult, ALU.mult)
                for c in range(NCH):
                    fw.mm(ps[3][:, c:c + 1], prod[:, c * 128:(c + 1) * 128], self.ones_b[0:64, 0:1])
                fw.copy(rks[:, :, h], ps[3][:, 0:NCH])
                tokm = self.A("tokm", [128, NCH, 5, 64], BF16)
                X0 = self.A("X0", [128, NCH, 128], BF16)
                for c in range(NCH):
                    cl = slice(c * 128, (c + 1) * 128)
                    pt = ps[4 + c % 2][:].bitcast(BF16)
                    srcs = (nar[:, c, 0, :], nbe[:, cl], ke[:, cl], vb[:, cl], nar[:, c, 1, :])
                    for qi, sv in enumerate(srcs):
                        fw.tr(pt[:, qi * 64:(qi + 1) * 64], sv, self.identb[0:64, 0:64])
                    fw.copy(tokm[:, c, :, :], pt[:, 0:320].rearrange("p (q k) -> p q k", q=5), eng="act")
                    fw.copy(X0[:, c, 0:64], tokm[:, c, 0, :], eng="pool")
                    fw.copy(vtok[:, c, h * 64:(h + 1) * 64], tokm[:, c, 3, :], eng="pool")
                for c in range(NCH):
                    cl = slice(c * 128, (c + 1) * 128)
                    AT = self.A("AT", [128, 512], BF16)
                    Lm = self.A("Lm", [128, 128], BF16)
                    narc = nar[:, c, :, :].rearrange("p a t -> p (a t)")
                    fw.mm(ps[6][:, 0:256], nbt[:, cl], narc)
                    fw.mm(ps[6][:, 256:512], kt[:, cl], narc)
                    fw.tt(AT[:], ps[6][:], self.mask4[:], ALU.mult)
                    fw.mm(ps[7][:, 0:128], nar[:, c, 0, :], nbt[:, cl])
                    fw.tt(Lm[:], ps[7][:, 0:128], self.mlow[:], ALU.mult)
                    fw.mm(ps[7][:, 128:192], AT[:, 256:384], tokm[:, c, 3, :])
                    fw.copy(X0[:, c, 64:128], ps[7][:, 128:192], eng="act")
                    Ncur = AT[:, 0:128]
                    Lcur = Lm[:]
                    Xcur = X0[:, c, :]
                    for j in range(7):
                        fw.mm(ps[6][:, 0:128], Ncur, Xcur)
                        Xn = self.A("X%d" % (j % 2 + 1), [128, 128], BF16)
                        fw.tt(Xn[:], ps[6][:, 0:128], Xcur, ALU.add)
                        Xcur = Xn[:]
                        if j < 6:
                            NL = self.A("NL%d" % (j % 2), [128, 256], BF16)
                            fw.mm(ps[7][:, 0:128], Lcur, Ncur)
                            fw.mm(ps[7][:, 128:256], Ncur, Lcur)
                            fw.copy(NL[:], ps[7][:, 0:256], eng="act")
                            Ncur = NL[:, 0:128]
                            Lcur = NL[:, 128:256]
                    Xa = Xcur[:, 0:64]
                    Xv = Xcur[:, 64:128]
                    MT = self.A("MT", [64, 64])
                    RpT = self.A("RpT", [64, 128], BF16)
                    fw.mm(ps[6][0:64, 0:64], Xa, tokm[:, c, 1, :])
                    fw.copy(MT[:], ps[6][0:64, 0:64], eng="act")
                    fw.mm(ps[7][0:64, 0:128], Xa, AT[:, 128:256], start=True, stop=False)
                    fw.mm(ps[7][0:64, 0:128], tokm[:, c, 4, :], self.identb[:], start=False, stop=True)
                    fw.copy(RpT[:], ps[7][0:64, 0:128], eng="act")
                    fw.mm(ps[6][:, 64:128], AT[:, 128:256], Xv, start=True, stop=False)
                    fw.mm(ps[6][:, 64:128], AT[:, 384:512], tokm[:, c, 3, :], start=False, stop=False)
                    fw.mm(ps[6][:, 64:128], RpT[:], self.rwSb[l][:, h, :], start=False, stop=True)
                    fw.copy(O_sb[:, c, h * 64:(h + 1) * 64], ps[6][:, 64:128], eng="act")
                    fw.mm(ps[7][0:64, 128:192], tokm[:, c, 1, :], Xv, start=True, stop=False)
                    fw.mm(ps[7][0:64, 128:192], tokm[:, c, 2, :], tokm[:, c, 3, :], start=False, stop=True)
                    G = self.A("G", [64, 64])
                    fw.copy(G[:], ps[7][0:64, 128:192], eng="act")
                    fw.mm(ps[7][0:64, 192:256], MT[:], self.rwS[l][:, h, :])
                    fw.tt(G[:], G[:], ps[7][0:64, 192:256], ALU.add)
                    fw.stt(self.rwS[l][:, h, :], self.rwS[l][:, h, :], epos[:, c * 128 + 127:c * 128 + 128], G[:], ALU.mult, ALU.add)
                    fw.copy(self.rwSb[l][:, h, :], self.rwS[l][:, h, :], eng="pool")
        pr = self.prow[l]
        for c in range(NCH):
            cl = slice(c * 128, (c + 1) * 128)
            fw.mm(ps[0][:], szg[:, cl], self.g2b[l][:])
            Ov = O_sb[:, c, :]
            O3 = Ov.rearrange("p (h k) -> p h k", h=8)
            s1 = self.A("gn_s1", [128, 8]); s2 = self.A("gn_s2", [128, 8]); sqt = self.A("gn_sq", [128, AW])
            fw.op("dve", lambda en, O3=O3: en.tensor_reduce(s1.h[:], O3.ap, AX.X, ALU.add), reads=[O3], writes=[s1[:]])
            fw.tt(sqt[:], Ov, Ov, ALU.mult, eng="pool")
            fw.op("dve", lambda en: en.tensor_reduce(s2.h[:], sqt.h[:].rearrange("p (h k) -> p h k", h=8), AX.X, ALU.add),
                  reads=[sqt[:]], writes=[s2[:]])
            fw.ts(s1[:], s1[:], 1.0 / 64, None, op0=ALU.mult)
            msq = self.A("gn_msq", [128, 8])
            fw.tt(msq[:], s1[:], s1[:], ALU.mult)
            fw.stt(s2[:], s2[:], 1.0 / 64, msq[:], ALU.mult, ALU.subtract)
            fw.ts(s2[:], s2[:], A_GN_EPS, None, op0=ALU.add)
            fw.act(s2[:], s2[:], AF.Sqrt)
            fw.op("dve", lambda en: en.reciprocal(s2.h[:], s2.h[:]), reads=[s2[:]], writes=[s2[:]])
            bc = lambda t_: V(t_.h[:].unsqueeze(2).to_broadcast([128, 8, 64]), t_)
            on = self.A("gn_on", [128, AW])
            on3 = on[:].rearrange("p (h k) -> p h k", h=8)
            fw.tt(on3, O3, bc(s1), ALU.subtract)
            fw.tt(on3, on3, bc(s2), ALU.mult)
            fw.tt(on[:], on[:], pr[:, 0:512], ALU.mult, eng="pool")
            fw.tt(on[:], on[:], pr[:, 512:1024], ALU.add, eng="pool")
            rkc = V(rks.h[:, c, :].unsqueeze(2).to_broadcast([128, 8, 64]), rks)
            fw.tt(sqt[:].rearrange("p (h k) -> p h k", h=8), vtok[:, c, :].rearrange("p (h k) -> p h k", h=8), rkc, ALU.mult)
            fw.tt(on[:], on[:], sqt[:], ALU.add, eng="pool")
            ya = self.A("ya", [128, AW], BF16)
            fw.tt(ya[:], on[:], ps[0][:], ALU.mult)
            pt = ps[1][:].bitcast(BF16)
            for c4 in range(4):
                fw.tr(pt[:, c4 * 128:(c4 + 1) * 128], ya[:, c4 * 128:(c4 + 1) * 128], self.identb[:])
            fw.copy(self.yaT[:, :, cl], pt[:, 0:512].rearrange("p (a t) -> p a t", a=4), eng="act")
        self.dump("yaT", self.yaT[:], [128, 4, TT], BF16)

    def stage_B(self, s, ti, l):
        fw = self.fw
        self.phase("B")
        if s == 0 and ti == 0 and l == 0 and self.nlayer > 1:
            self.convert_layer(1)
        ps = self.ps
        wv = lambda t_: t_.h.ap().rearrange("(kc p) n -> p kc n", p=128)
        fw.dma(self.ropet[:], V(self.rope_d.h.ap()[:, :, ti * TT:(ti + 1) * TT], self.rope_d))
        cosv = self.ropet[:, 0, :]
        sinv = self.ropet[:, 1, :]
        qT = self.A("qT", [128, 4, TT], BF16)
        kT = self.A("kT", [128, 4, TT], BF16)
        vtk = self.A("vtk", [128, 4, 1024], BF16)
        gtk = self.A("gtk", [128, 4, 1024], BF16)
        for dst, off in ((qT, O_BQ), (kT, O_BK)):
            ws = self.slot()
            fw.dma(ws[:], V(wv(self.w_in_s[l])[:, :, off:off + 512], self.w_in_s[l]))
            for h in range(4):
                pp = ps[h % 2]
                self.proj_fm(pp[:], lambda kc: ws[:, kc, h * 128:(h + 1) * 128], 128)
                qb = self.A("qb", [128, TT], BF16)
                qc = self.A("qc", [128, TT])
                tokq = fw.copy(qb[:], pp[:], eng="act")
                fw.tt(qc[:], pp[:], cosv, ALU.mult, extra=[tokq])
                p2 = ps[2 + h % 2]
                fw.mm(p2[:], self.pswap[:], qb[:])
                qs = self.A("qs", [128, TT])
                fw.tt(qs[:], p2[:], sinv, ALU.mult)
                fw.tt(dst[:, h, :], qc[:], qs[:], ALU.add, eng="pool")
        for dst, off, is_g in ((vtk, O_BV, False), (gtk, O_BG, True)):
            for half in range(2):
                ws = self.slot()
                fw.dma(ws[:], V(wv(self.w_in_s[l])[:, :, off + half * 512:off + half * 512 + 512], self.w_in_s[l]))
                for b in range(4):
                    pp = ps[b]
                    for kc in range(8):
                        fw.mm(pp[:], self.xb[:, kc, b * 128:(b + 1) * 128], ws[:, kc, :], start=(kc == 0), stop=(kc == 7))
                    if is_g:
                        fw.act(dst[:, b, half * 512:(half + 1) * 512], pp[:], AF.Silu)
                    else:
                        fw.copy(dst[:, b, half * 512:(half + 1) * 512], pp[:], eng="act")
        rc = self.retcols
        for c in range(NCH):
            cl = slice(c * 128, (c + 1) * 128)
            sc = self.A("sc", [128, 4, 128], BF16)
            ktk = self.A("ktk", [128, 4, 128], BF16)
            for h in range(4):
                fw.mm(ps[4][:, h * 128:(h + 1) * 128], kT[:, h, cl], qT[:, h, cl])
            fw.tt(sc[:].rearrange("p h t -> p (h t)"), ps[4][:], self.dmask[:].rearrange("p h t -> p (h t)"), ALU.mult)
            ptb = ps[5][:].bitcast(BF16)
            for h in range(4):
                fw.tr(ptb[:, h * 128:(h + 1) * 128], kT[:, h, cl], self.identb[:])
            for h in range(4):
                fw.ts(ktk[:, h, :], ptb[:, h * 128:(h + 1) * 128], rc[:, 4 + h:5 + h], None, op0=ALU.mult)
            o_sb = self.A("o_sb", [128, 4, 256])
            for h in range(4):
                pi = ps[6 + h % 2]
                vh = vtk[:, c, h * 256:(h + 1) * 256]
                fw.mm(pi[:, 0:256], sc[:, h, :], vh)
                fw.mm(pi[:, 256:512], qT[:, h, cl], self.rtRb[l][:, h, :])
                fw.copy(o_sb[:, h, :], pi[:, 0:256], eng="act")
                fw.stt(o_sb[:, h, :], pi[:, 256:512], rc[:, h:h + 1], o_sb[:, h, :], ALU.mult, ALU.add)
                pst = ps[h % 2]
                fw.mm(pst[:, 0:256], ktk[:, h, :], vh)
                fw.stt(self.rtR[l][:, h, :], self.rtR[l][:, h, :], GAMMAS[h] ** 128, pst[:, 0:256], ALU.mult, ALU.add)
                fw.copy(self.rtRb[l][:, h, :], self.rtR[l][:, h, :], eng="pool")
            st = self.A("bnst", [128, 4, 6])
            mv = self.A("bnmv", [128, 4, 2])
            for h in range(4):
                fw.op("dve", lambda en, h=h: en.bn_stats(st.h[:, h, :], o_sb.h[:, h, :]), reads=[o_sb[:]], writes=[st[:]])
                fw.op("dve", lambda en, h=h: en.bn_aggr(mv.h[:, h, :], st.h[:, h, :]), reads=[st[:]], writes=[mv[:]])
            rstd = self.A("brstd", [128, 4])
            fw.ts(rstd[:], mv[:, :, 1], B_GN_EPS, None, op0=ALU.add)
            fw.act(rstd[:], rstd[:], AF.Sqrt)
            fw.op("dve", lambda en: en.reciprocal(rstd.h[:], rstd.h[:]), reads=[rstd[:]], writes=[rstd[:]])
            for h in range(4):
                fw.ts(o_sb[:, h, :], o_sb[:, h, :], mv[:, h, 0:1], rstd[:, h:h + 1], op0=ALU.subtract, op1=ALU.mult,
                      eng="pool" if h % 2 else "dve")
            yb = self.A("yb", [128, 1024], BF16)
            fw.tt(yb[:], o_sb[:].rearrange("p h e -> p (h e)"), gtk[:, c, :], ALU.mult)
            for half in range(2):
                pt = ps[2 + half][:].bitcast(BF16)
                for i in range(4):
                    fc = half * 4 + i
                    fw.tr(pt[:, i * 128:(i + 1) * 128], yb[:, fc * 128:(fc + 1) * 128], self.identb[:])
                fw.copy(self.ybT[:, half * 4:half * 4 + 4, cl], pt[:, 0:512].rearrange("p (a t) -> p a t", a=4), eng="act")
        self.dump("ybT", self.ybT[:], [128, 8, TT], BF16)

    def stage_C(self, s, ti, l):
        fw = self.fw
        self.phase("C")
        ps = self.ps
        pc = self.pcol[l]
        ws = self.slot()
        fw.dma(ws[:], V(self.w_in_s[l].h.ap().rearrange("(kc p) n -> p kc n", p=128)[:, :, O_CU:O_CU + 512], self.w_in_s[l]))
        uT = self.A("uT", [128, 4, TT])
        uTb = self.A("uTb", [128, 4, TT], BF16)
        for k4 in range(4):
            self.proj_fm(ps[k4][:], lambda kc: ws[:, kc, k4 * 128:(k4 + 1) * 128], 128)
            fw.copy(uT[:, k4, :], ps[k4][:], eng="act")
            fw.copy(uTb[:, k4, :], uT[:, k4, :], eng="dve")
        st = self.s5st[l]
        SL = S5L
        for k in range(TT // SL):
            kcols = slice(k * SL, (k + 1) * SL)
            for gp in range(16):
                k4 = gp // 4
                q = gp % 4
                hh = q // 2
                e = q % 2
                rows = slice(64 * hh, 64 * hh + 64)
                cosv = self.s5cos[l][:, gp, :]
                sinv = self.s5sin[l][:, gp, :]
                pb = ps[4 + gp % 2]
                fw.mm(pb[:, 0:SL], self.s5B[l][rows, 0, k4, e, :], uTb[rows, k4, kcols])
                fw.mm(pb[:, SL:2 * SL], self.s5B[l][rows, 1, k4, e, :], uTb[rows, k4, kcols])
                zr = self.A("zr", [128, SL]); zi = self.A("zi", [128, SL]); t1 = self.A("t1", [128, SL]); t2 = self.A("t2", [128, SL])
                fw.tt(zr[:], pb[:, 0:SL], cosv, ALU.mult)
                fw.tt(t1[:], pb[:, SL:2 * SL], sinv, ALU.mult)
                fw.tt(zi[:], pb[:, SL:2 * SL], cosv, ALU.mult)
                fw.tt(t2[:], pb[:, 0:SL], sinv, ALU.mult)
                fw.tt(zr[:], zr[:], t1[:], ALU.add, eng="pool")
                fw.tt(zi[:], zi[:], t2[:], ALU.subtract, eng="pool")
                rho = V(self.s5rho[l].h[:, gp:gp + 1].to_broadcast([128, SL]), self.s5rho[l])
                zcr = self.A("zcr", [128, SL]); zci = self.A("zci", [128, SL])
                fw.op("dve", lambda en, rho=rho, zr=zr, zcr=zcr, gp=gp: en.tensor_tensor_scan(zcr.h[:], rho.ap, zr.h[:], st.h[:, 0, gp:gp + 1], ALU.mult, ALU.add),
                      reads=[rho, zr[:], st[:]], writes=[zcr[:]])
                fw.op("dve", lambda en, rho=rho, zi=zi, zci=zci, gp=gp: en.tensor_tensor_scan(zci.h[:], rho.ap, zi.h[:], st.h[:, 1, gp:gp + 1], ALU.mult, ALU.add),
                      reads=[rho, zi[:], st[:]], writes=[zci[:]])
                xr = self.A("xr", [128, SL]); xi = self.A("xi", [128, SL])
                fw.tt(xr[:], zcr[:], cosv, ALU.mult, eng="pool")
                fw.tt(t1[:], zci[:], sinv, ALU.mult, eng="pool")
                fw.tt(xi[:], zcr[:], sinv, ALU.mult)
                fw.tt(t2[:], zci[:], cosv, ALU.mult)
                fw.tt(xr[:], xr[:], t1[:], ALU.subtract, eng="pool")
                fw.tt(xi[:], xi[:], t2[:], ALU.add)
                xrb = self.A("xrb", [128, SL], BF16); xib = self.A("xib", [128, SL], BF16)
                fw.copy(xrb[:], xr[:], eng="act")
                fw.copy(xib[:], xi[:], eng="act")
                fw.copy(st[:, 0, gp:gp + 1], xr[:, SL - 1:SL], eng="pool")
                fw.copy(st[:, 1, gp:gp + 1], xi[:, SL - 1:SL], eng="dve")
                if k == 0 and gp == 5:
                    self.dump("d_zr", zr[:], [128, SL]); self.dump("d_zcr", zcr[:], [128, SL]); self.dump("d_xr", xr[:], [128, SL])
                    self.dump("d_B", self.s5B[l][:, 0, 1, :, :], [128, 2, 128], BF16); self.dump("d_C", self.s5C[l][:, 0, 5, :], [128, 128], BF16)
                    self.dump("d_cos", self.s5cos[l][:, 5, :], [128, SL]); self.dump("d_rho", self.s5rho[l][:], [128, 16])
                py = ps[k4]
                fw.mm(py[:, kcols], self.s5C[l][:, 0, gp, :], xrb[:], start=(q == 0), stop=False)
                fw.mm(py[:, kcols], self.s5C[l][:, 1, gp, :], xib[:], start=False, stop=(q == 3))
        yg = self.A("yg", [128, 4, TT])
        ygb = self.A("ygb", [128, 4, TT], BF16)
        if "s5y" in self.dbg:
            s5y = self.A("s5y", [128, 4, TT])
            for k4 in range(4):
                fw.copy(s5y[:, k4, :], ps[k4][:], eng="act")
            self.dump("s5y", s5y[:], [128, 4, TT])
        for k4 in range(4):
            fw.stt(yg[:, k4, :], uT[:, k4, :], pc[:, PC_CD + k4:PC_CD + k4 + 1], ps[k4][:], ALU.mult, ALU.add)
            fw.act(yg[:, k4, :], yg[:, k4, :], AF.Gelu_apprx_tanh)
            fw.copy(ygb[:, k4, :], yg[:, k4, :], eng="pool")
        for oc in range(4):
            pg = ps[4 + oc % 2]
            for kc in range(4):
                fw.mm(pg[:], self.wglu[l][:, kc, oc * 128:(oc + 1) * 128], ygb[:, kc, :], start=(kc == 0), stop=(kc == 3))
            sg = self.A("sgl", [128, TT])
            fw.act(sg[:], pg[:], AF.Sigmoid, bias=pc[:, PC_CBG + oc:PC_CBG + oc + 1])
            fw.tt(self.ycT[:, oc, :], yg[:, oc, :], sg[:], ALU.mult)
        self.dump("ycT", self.ycT[:], [128, 4, TT], BF16)

    def stage_M(self, s, ti, l):
        fw = self.fw
        self.phase("M")
        ps = self.ps
        pc = self.pcol[l]
        wv = lambda t_: t_.h.ap().rearrange("(kc p) n -> p kc n", p=128)
        y1 = self.A("y1", [128, 8, TT])
        mT = self.A("mT", [128, 8, TT], BF16)
        gates = self.A("gates", [128, 3, 4, TT])
        for ch in range(2):
            for b in range(3):
                ws = self.slot()
                off = O_GATE + b * 1024 + ch * 512
                fw.dma(ws[:], V(wv(self.w_in_s[l])[:, :, off:off + 512], self.w_in_s[l]))
                for f4 in range(4):
                    pp = ps[f4 % 2]
                    self.proj_fm(pp[:], lambda kc: ws[:, kc, f4 * 128:(f4 + 1) * 128], 128)
                    col = PC_BG + b * 8 + ch * 4 + f4
                    fw.act(gates[:, b, f4, :], pp[:], AF.Sigmoid, bias=pc[:, col:col + 1])
            w0 = self.slot()
            w1 = self.slot()
            fw.dma(w0[:], V(wv(self.w_br_s[l])[:, 0:8, ch * 512:(ch + 1) * 512], self.w_br_s[l]))
            fw.dma(w1[:], V(wv(self.w_br_s[l])[:, 8:16, ch * 512:(ch + 1) * 512], self.w_br_s[l]))
            for f4 in range(4):
                fc = ch * 4 + f4
                fcs = slice(f4 * 128, (f4 + 1) * 128)
                pa, pb_, pcc = ps[2], ps[3], ps[4]
                for kc in range(4):
                    fw.mm(pa[:], w0[:, kc, fcs], self.yaT[:, kc, :], start=(kc == 0), stop=(kc == 3))
                for kc in range(8):
                    wsl = w0[:, 4 + kc, fcs] if kc < 4 else w1[:, kc - 4, fcs]
                    fw.mm(pb_[:], wsl, self.ybT[:, kc, :], start=(kc == 0), stop=(kc == 7))
                for kc in range(4):
                    fw.mm(pcc[:], w1[:, 4 + kc, fcs], self.ycT[:, kc, :], start=(kc == 0), stop=(kc == 3))
                m1 = self.A("m1", [128, TT]); m2 = self.A("m2", [128, TT])
                fw.tt(m1[:], pa[:], gates[:, 0, f4, :], ALU.mult)
                fw.tt(m2[:], pb_[:], gates[:, 1, f4, :], ALU.mult)
                fw.tt(m1[:], m1[:], m2[:], ALU.add, eng="pool")
                fw.tt(m2[:], pcc[:], gates[:, 2, f4, :], ALU.mult)
                fw.tt(mT[:, fc, :], m1[:], m2[:], ALU.add, eng="pool")
        for ch in range(2):
            ws = self.slot()
            fw.dma(ws[:], V(wv(self.w_out_s[l])[:, :, ch * 512:(ch + 1) * 512], self.w_out_s[l]))
            for f4 in range(4):
                fc = ch * 4 + f4
                pp = ps[f4 % 2]
                for kc in range(8):
                    fw.mm(pp[:], ws[:, kc, f4 * 128:(f4 + 1) * 128], mT[:, kc, :], start=(kc == 0), stop=(kc == 7))
                fw.stt(y1[:, fc, :], self.xf[:, fc, :], ALPHA, pp[:], ALU.mult, ALU.add)
        self.phase("LN1")
        y1b = self.A("y1", [128, 8, TT])
        self.layer_norm(y1b, l, lambda kc: pc[:, PC_LN + kc:PC_LN + kc + 1], lambda kc: pc[:, PC_LN + 8 + kc:PC_LN + 9 + kc], "ln1")
        self.dump("x1", self.xf[:], [128, 8, TT])

    def stage_F(self, s, ti, l):
        fw = self.fw
        self.phase("F")
        ps = self.ps
        pc = self.pcol[l]
        wv = lambda t_: t_.h.ap().rearrange("(kc p) n -> p kc n", p=128)
        y2 = self.A("y2", [128, 8, TT])
        hT = self.A("hT", [128, 32, TT], BF16)
        for jg in range(8):
            ws = self.slot()
            fw.dma(ws[:], V(wv(self.w_ff1_s[l])[:, :, jg * 512:(jg + 1) * 512], self.w_ff1_s[l]))
            for jj in range(4):
                j = jg * 4 + jj
                pp = ps[j % 4]
                self.proj_fm(pp[:], lambda kc: ws[:, kc, jj * 128:(jj + 1) * 128], 128)
                rt = self.A("rt%d" % (j % 2), [128, TT])
                fw.act(rt[:], pp[:], AF.Relu)
                fw.tt(hT[:, j, :], rt[:], rt[:], ALU.mult, eng="pool" if j % 2 else "dve")
        for fh in range(2):
            for jg in range(4):
                ws = self.slot()
                fw.dma(ws[:], V(wv(self.w_ff2_s[l])[:, jg * 8:(jg + 1) * 8, fh * 512:(fh + 1) * 512], self.w_ff2_s[l]))
                for i in range(8):
                    j = jg * 8 + i
                    for f4 in range(4):
                        fw.mm(ps[4 + f4][:], ws[:, i, f4 * 128:(f4 + 1) * 128], hT[:, j, :], start=(j == 0), stop=(j == 31))
            for f4 in range(4):
                fc = fh * 4 + f4
                fw.stt(y2[:, fc, :], self.xf[:, fc, :], ALPHA, ps[4 + f4][:], ALU.mult, ALU.add)
        self.phase("LN2")
        y2b = self.A("y2", [128, 8, TT])
        self.layer_norm(y2b, l, lambda kc: pc[:, PC_LN + 16 + kc:PC_LN + 17 + kc], lambda kc: pc[:, PC_LN + 24 + kc:PC_LN + 25 + kc], "ln2")
        self.dump("x2", self.xf[:], [128, 8, TT])


_CONSTS = None


def _prep(inputs):
    global _CONSTS
    if _CONSTS is None:
        _CONSTS = host_consts()
    inp = {k: np.asarray(v) for k, v in inputs.items()}
    pcol, prow, s5b, s5c = host_params(inp)
    shared = dict(_CONSTS)
    shared.update(pcol=pcol, prow=prow, s5b=s5b, s5c=s5c)
    for k in ("w_in", "w_branch", "w_out", "w_ff1", "w_ff2", "a_w2", "a_a2", "a_g2", "c_w_glu"):
        shared[k] = np.ascontiguousarray(inp[k], dtype=np.float32)
    return inp, shared


def kernel(**inputs):
    inp, shared = _prep(inputs)
    x = np.ascontiguousarray(inp["x"], dtype=np.float32)
    ncores = 8
    per = x.shape[0] // ncores
    prog = Prog(nseq=per, ntile=SEQ // TT, nlayer=L)
    nc = prog.build()
    in_maps = []
    for c in range(ncores):
        m = dict(shared)
        m["x"] = x[c * per:(c + 1) * per]
        in_maps.append(m)
    res = run_bass_kernel_spmd(nc, in_maps, core_ids=list(range(ncores)))
    out = np.concatenate([np.asarray(r["y"]) for r in res.results], axis=0)
    return out.astype(np.float32)
```
